# Optimizing a Trainium2 kernel written in Bass

```python
import jax
import jax.numpy as jnp
from jax import lax
import numpy as np

D_MODEL = 2048
BATCH = 16
SEQ = 256
DEPTH = 2
DEC_BATCH = 8
DEC_SEQ = 1024
PAST_LEN = 512

GRID_W = 64
N_BRANCH = 4
BRANCH_W = D_MODEL // N_BRANCH
NA_HEAD_DIM = 64
NA_HEADS = BRANCH_W // NA_HEAD_DIM
NA_WIN_ROWS = 8
NA_WIN_COLS = 16
RPB_ROWS = 2 * NA_WIN_ROWS - 1
RPB_COLS = 2 * NA_WIN_COLS - 1
ROPE_BASE = 10000.0
ATTN_BLOCK = 128
GMLP_CHUNK = 128
GMLP_GROUPS = 4
GMLP_GROUP_CH = BRANCH_W // GMLP_GROUPS
FNET_GROUPS = 4
FNET_GROUP_CH = BRANCH_W // FNET_GROUPS
POOL_WINDOWS = (2, 4, 8, 16)
POOL_GROUP_CH = BRANCH_W // len(POOL_WINDOWS)
MIX_WIDTH = N_BRANCH * BRANCH_W
D_FF = 4 * D_MODEL
EPS = 1e-6
NEG_INF = -1e30
IN_COLS = 7 * BRANCH_W + N_BRANCH * D_MODEL
SPLIT_POINTS = tuple(BRANCH_W * i for i in range(1, 8))

kernel_name = 'hybrid_diffusion_trunk_step'


def rmsnorm(x, g):
    xf = x.astype(jnp.float32)
    y = xf * lax.rsqrt(jnp.mean(xf * xf, axis=-1, keepdims=True) + EPS)
    return (y * g.astype(jnp.float32)).astype(x.dtype)


def axial_angles(n):
    t = jnp.arange(n)
    pos = jnp.stack([t // GRID_W, t % GRID_W], axis=-1).astype(jnp.float32)
    half = NA_HEAD_DIM // 2
    inv = 1.0 / (ROPE_BASE ** (jnp.arange(0, half, 2, dtype=jnp.float32) / half))
    return pos[:, :, None] * inv


def apply_axial_rope(x, ang):
    B, N, H, Dh = x.shape
    xs = x.astype(jnp.float32).reshape(B, N, H, 2, 2, Dh // 4)
    cos = jnp.cos(ang)[None, :, None]
    sin = jnp.sin(ang)[None, :, None]
    x1 = xs[..., 0, :]
    x2 = xs[..., 1, :]
    out = jnp.stack([x1 * cos - x2 * sin, x1 * sin + x2 * cos], axis=-2)
    return out.reshape(x.shape).astype(x.dtype)


def context_attention(q, k, v):
    B, S, H, Dh = q.shape
    nb = S // ATTN_BLOCK
    scale = Dh ** -0.5
    kt = k.transpose(0, 2, 1, 3)
    vt = v.transpose(0, 2, 1, 3)
    qb = q.reshape(B, nb, ATTN_BLOCK, H, Dh).transpose(1, 0, 2, 3, 4)

    def block(qi):
        s = jnp.einsum('bqhd,bhkd->bhqk', qi, kt).astype(jnp.float32) * scale
        p = jax.nn.softmax(s, axis=-1).astype(vt.dtype)
        return jnp.einsum('bhqk,bhkd->bqhd', p, vt)

    o = lax.map(block, qb)
    return o.transpose(1, 0, 2, 3, 4).reshape(B, S, H * Dh), kt, vt


def latent_neighbourhood_attention(q, k, v, k_ctx, v_ctx, rpb):
    B, N, H, Dh = q.shape
    rows = N // GRID_W
    wr = min(NA_WIN_ROWS, rows)
    scale = Dh ** -0.5
    ang = axial_angles(N)
    qr = apply_axial_rope(q, ang).reshape(B, rows, GRID_W, H, Dh)
    kr = apply_axial_rope(k, ang).reshape(B, rows, GRID_W, H, Dh)
    qp = q.reshape(B, rows, GRID_W, H, Dh)
    vg = v.reshape(B, rows, GRID_W, H, Dh)
    cq = jnp.arange(GRID_W)
    cs = jnp.clip(cq - NA_WIN_COLS // 2, 0, GRID_W - NA_WIN_COLS)
    kc = jnp.arange(GRID_W)
    col_ok = (kc[None, :] >= cs[:, None]) & (kc[None, :] < cs[:, None] + NA_WIN_COLS)
    col_mask = jnp.where(col_ok, 0.0, NEG_INF).astype(jnp.float32)
    dc_idx = jnp.clip(kc[None, :] - cq[:, None], -(NA_WIN_COLS - 1), NA_WIN_COLS - 1) + NA_WIN_COLS - 1
    rpb_c = rpb[:, :, dc_idx]

    def row_block(r):
        rs = jnp.clip(r - NA_WIN_ROWS // 2, 0, rows - wr)
        q_r = lax.dynamic_index_in_dim(qr, r, axis=1, keepdims=False)
        qp_r = lax.dynamic_index_in_dim(qp, r, axis=1, keepdims=False)
        k_b = lax.dynamic_slice_in_dim(kr, rs, wr, axis=1)
        v_b = lax.dynamic_slice_in_dim(vg, rs, wr, axis=1)
        dr_idx = rs + jnp.arange(wr) - r + NA_WIN_ROWS - 1
        bias = jnp.take(rpb_c, dr_idx, axis=1).transpose(0, 2, 1, 3).astype(jnp.float32)
        bias = bias + col_mask[None, :, None, :]
        s_loc = jnp.einsum('bqhd,bwkhd->bhqwk', q_r, k_b).astype(jnp.float32) * scale + bias[None]
        s_ctx = jnp.einsum('bqhd,bhld->bhql', qp_r, k_ctx).astype(jnp.float32) * scale
        s = jnp.concatenate([s_loc.reshape(B, H, GRID_W, wr * GRID_W), s_ctx], axis=-1)
        p = jax.nn.softmax(s, axis=-1).astype(v.dtype)
        p_loc = p[..., :wr * GRID_W].reshape(B, H, GRID_W, wr, GRID_W)
        p_ctx = p[..., wr * GRID_W:]
        return (jnp.einsum('bhqwk,bwkhd->bqhd', p_loc, v_b)
                + jnp.einsum('bhql,bhld->bqhd', p_ctx, v_ctx))

    o = lax.map(row_block, jnp.arange(rows))
    return o.transpose(1, 0, 2, 3, 4).reshape(B, N, H * Dh)


def spatial_gating(u, vg, norm_g, w_s, b_s):
    B, N, _ = u.shape
    u = jax.nn.gelu(u)
    vg = rmsnorm(jax.nn.gelu(vg), norm_g)
    vc = vg.reshape(B, N // GMLP_CHUNK, GMLP_CHUNK, GMLP_GROUPS, GMLP_GROUP_CH)
    mixed = jnp.einsum('gpq,bnqgc->bnpgc', w_s, vc) + b_s.T[None, None, :, :, None]
    return u * mixed.reshape(B, N, BRANCH_W)


def fourier_mix(xf):
    B, N, _ = xf.shape
    z = xf.astype(jnp.float32).reshape(B, N, FNET_GROUPS, FNET_GROUP_CH)
    y = jnp.fft.fft2(z, axes=(1, 3), norm='ortho').real
    return y.reshape(B, N, BRANCH_W).astype(xf.dtype)


def multiscale_pool(xp, w_pool, pool_scale):
    B, N, _ = xp.shape
    xf = xp.astype(jnp.float32).reshape(B, N, len(POOL_WINDOWS), POOL_GROUP_CH)
    csum = jnp.concatenate([jnp.zeros_like(xf[:, :1]), jnp.cumsum(xf, axis=1)], axis=1)
    t = jnp.arange(N)
    means = []
    for gi, w in enumerate(POOL_WINDOWS):
        lo = jnp.clip(t - w // 2, 0, N - 1)
        hi = jnp.clip(t + w // 2 - 1, 0, N - 1)
        cg = csum[:, :, gi]
        s = jnp.take(cg, hi + 1, axis=1) - jnp.take(cg, lo, axis=1)
        means.append(s / (hi - lo + 1).astype(jnp.float32)[None, :, None])
    pooled = (jnp.stack(means, axis=2) - xf).astype(xp.dtype)
    y = jnp.einsum('bngc,gcd->bngd', pooled, w_pool).reshape(B, N, BRANCH_W)
    return y * pool_scale


def token_mixers(h, ctx_kv, w_in, b_gate, q_norm_g, k_norm_g, rpb, gmlp_norm_g,
                 w_spatial, b_spatial, w_pool, pool_scale, w_branch, w_out):
    B, N, _ = h.shape
    q, k, v, u, vg, xf, xp, gl = jnp.split(h @ w_in, SPLIT_POINTS, axis=-1)
    q = rmsnorm(q.reshape(B, N, NA_HEADS, NA_HEAD_DIM), q_norm_g)
    k = rmsnorm(k.reshape(B, N, NA_HEADS, NA_HEAD_DIM), k_norm_g)
    v = v.reshape(B, N, NA_HEADS, NA_HEAD_DIM)
    if ctx_kv is None:
        a, kt, vt = context_attention(q, k, v)
        kv = (kt, vt)
    else:
        a = latent_neighbourhood_attention(q, k, v, ctx_kv[0], ctx_kv[1], rpb)
        kv = None
    branches = (a,
                spatial_gating(u, vg, gmlp_norm_g, w_spatial, b_spatial),
                fourier_mix(xf),
                multiscale_pool(xp, w_pool, pool_scale))
    gates = jax.nn.sigmoid(gl + b_gate).reshape(B, N, N_BRANCH, D_MODEL)
    merged = gates[:, :, 0] * (branches[0] @ w_branch[0:BRANCH_W])
    for i in range(1, N_BRANCH):
        merged = merged + gates[:, :, i] * (branches[i] @ w_branch[i * BRANCH_W:(i + 1) * BRANCH_W])
    return merged @ w_out, kv


def trunk_layer(x, cond, ctx_kv, norm1_g, w_in, b_gate, q_norm_g, k_norm_g, rpb, gmlp_norm_g,
                w_spatial, b_spatial, w_pool, pool_scale, w_branch, w_out, norm2_g,
                w_mlp1, w_mlp2, w_ada, b_ada):
    m = jax.nn.silu(cond) @ w_ada + b_ada
    sh1, sc1, g1, sh2, sc2, g2 = [t[:, None, :] for t in jnp.split(m, 6, axis=-1)]
    h = rmsnorm(x, norm1_g) * (1 + sc1) + sh1
    mix, kv = token_mixers(h, ctx_kv, w_in, b_gate, q_norm_g, k_norm_g, rpb, gmlp_norm_g,
                           w_spatial, b_spatial, w_pool, pool_scale, w_branch, w_out)
    x = x + g1 * mix
    h2 = rmsnorm(x, norm2_g) * (1 + sc2) + sh2
    x = x + g2 * (jnp.square(jax.nn.relu(h2 @ w_mlp1)) @ w_mlp2)
    return x, kv


def setup_inputs(seed: int = 0) -> dict:
    key = jax.random.key(seed)
    ks = jax.random.split(key, 24)
    f32 = jnp.float32
    nrm = lambda k, shape, s: jax.random.normal(k, shape, f32) * s
    kv_shape = (DEC_BATCH, DEPTH, NA_HEADS, PAST_LEN, NA_HEAD_DIM)
    return {
        'x_prompt': nrm(ks[0], (BATCH, SEQ, D_MODEL), 1.0),
        'x_sample': nrm(ks[1], (DEC_BATCH, DEC_SEQ, D_MODEL), 1.0),
        'cache_k': nrm(ks[2], kv_shape, 1.0),
        'cache_v': nrm(ks[3], kv_shape, 1.0),
        'c': nrm(ks[4], (DEC_BATCH, D_MODEL), 1.0),
        'c_ctx': nrm(ks[5], (D_MODEL,), 1.0),
        'norm1_g': 1.0 + nrm(ks[6], (DEPTH, D_MODEL), 0.02),
        'w_in': nrm(ks[7], (DEPTH, D_MODEL, IN_COLS), D_MODEL ** -0.5),
        'b_gate': nrm(ks[8], (DEPTH, N_BRANCH * D_MODEL), 0.02),
        'q_norm_g': 1.0 + nrm(ks[9], (DEPTH, NA_HEAD_DIM), 0.02),
        'k_norm_g': 1.0 + nrm(ks[10], (DEPTH, NA_HEAD_DIM), 0.02),
        'rpb': nrm(ks[11], (DEPTH, NA_HEADS, RPB_ROWS, RPB_COLS), 0.1),
        'gmlp_norm_g': 1.0 + nrm(ks[12], (DEPTH, BRANCH_W), 0.02),
        'w_spatial': nrm(ks[13], (DEPTH, GMLP_GROUPS, GMLP_CHUNK, GMLP_CHUNK), GMLP_CHUNK ** -0.5),
        'b_spatial': 1.0 + nrm(ks[14], (DEPTH, GMLP_GROUPS, GMLP_CHUNK), 0.02),
        'w_pool': nrm(ks[15], (DEPTH, len(POOL_WINDOWS), POOL_GROUP_CH, POOL_GROUP_CH), POOL_GROUP_CH ** -0.5),
        'pool_scale': 1.0 + nrm(ks[16], (DEPTH, BRANCH_W), 0.02),
        'w_branch': nrm(ks[17], (DEPTH, MIX_WIDTH, D_MODEL), BRANCH_W ** -0.5),
        'w_out': nrm(ks[18], (DEPTH, D_MODEL, D_MODEL), D_MODEL ** -0.5),
        'norm2_g': 1.0 + nrm(ks[19], (DEPTH, D_MODEL), 0.02),
        'w_mlp1': nrm(ks[20], (DEPTH, D_MODEL, D_FF), D_MODEL ** -0.5),
        'w_mlp2': nrm(ks[21], (DEPTH, D_FF, D_MODEL), D_FF ** -0.5),
        'w_ada': nrm(ks[22], (DEPTH, D_MODEL, 6 * D_MODEL), D_MODEL ** -0.5),
        'b_ada': nrm(ks[23], (DEPTH, 6 * D_MODEL), 0.02),
    }


def reference(x_prompt, x_sample, cache_k, cache_v, c, c_ctx, norm1_g, w_in, b_gate,
              q_norm_g, k_norm_g, rpb, gmlp_norm_g, w_spatial, b_spatial, w_pool, pool_scale,
              w_branch, w_out, norm2_g, w_mlp1, w_mlp2, w_ada, b_ada):
    layer_weights = (norm1_g, w_in, b_gate, q_norm_g, k_norm_g, rpb, gmlp_norm_g, w_spatial,
                     b_spatial, w_pool, pool_scale, w_branch, w_out, norm2_g, w_mlp1, w_mlp2,
                     w_ada, b_ada)
    yp = x_prompt
    new_k = []
    new_v = []
    for l in range(DEPTH):
        yp, (k_l, v_l) = trunk_layer(yp, c_ctx[None, :], None, *[w[l] for w in layer_weights])
        new_k.append(k_l)
        new_v.append(v_l)
    ys = x_sample
    for l in range(DEPTH):
        ys, _ = trunk_layer(ys, c, (cache_k[:, l], cache_v[:, l]), *[w[l] for w in layer_weights])
    new_cache_k = jnp.stack(new_k, axis=1)
    new_cache_v = jnp.stack(new_v, axis=1)
    return (yp, ys, new_cache_k, new_cache_v)
```

```python
import numpy as np
import ml_dtypes
import concourse.bass as bass
import concourse.mybir as mybir
from concourse.bass_utils import run_bass_kernel_spmd

F32 = mybir.dt.float32
BF16 = mybir.dt.bfloat16
AF = mybir.ActivationFunctionType
ALU = mybir.AluOpType

D = 2048
DEPTH = 2
NCH = 16
BW = 512
NH = 8
HD = 64
DFF = 8192
IN_COLS = 7 * BW + 4 * D
PAST = 512
GRID_W = 64
EPS = 1e-6
NEG = -30000.0
N_CORES = 8
NDMA = 16
GELU_C = 1.5957691216057308

TRUST_SAME = ('pe',)


class Tracker:
    def __init__(self, nc):
        self.nc = nc
        self.eng = {'pe': nc.tensor, 'act': nc.scalar, 'dve': nc.vector,
                    'pool': nc.gpsimd, 'sp': nc.sync}
        self.sem = {}
        self.cnt = {}
        self.nsem = 0
        self.waited = {e: {} for e in self.eng}
        self.last_w = {}
        self.readers = {}
        for e in ('pe', 'act', 'dve', 'pool'):
            self._new_sem(e)
        self.dma_sems = [nc.alloc_semaphore(f"dq{i}") for i in range(NDMA)]
        self.dma_cnt = [0] * NDMA
        self.dma_last = [None] * NDMA
        self.dma_rr = 0
        self.dma_rr_sp = 0
        self.bank_rr = 0
        self.held = set()
        self.all_tokens = {}

    def _new_sem(self, e):
        self.sem[e] = self.nc.alloc_semaphore(f"s{e}{self.nsem}")
        self.nsem += 1
        self.cnt[e] = 0

    def epoch(self):
        for e in ('pe', 'act', 'dve', 'pool'):
            if self.cnt[e] > 0:
                self._new_sem(e)

    def bank(self, hold=False):
        for _ in range(8):
            b = self.bank_rr
            self.bank_rr = (b + 1) % 8
            if b not in self.held:
                break
        else:
            raise RuntimeError("no free PSUM bank")
        if hold:
            self.held.add(b)
        return b

    def release(self, b):
        self.held.discard(b)

    def _wait(self, e, tok):
        sem, val, _ = tok
        w = self.waited[e]
        if w.get(sem.num, 0) >= val:
            return
        self.eng[e].wait_ge(sem, val)
        w[sem.num] = val

    @staticmethod
    def _expand(keys):
        out = []
        for k in keys:
            if isinstance(k, tuple) and len(k) == 2 and k[0] == 'sm':
                out.append(('smh', k[1], 0))
                out.append(('smh', k[1], 1))
            else:
                out.append(k)
        return out

    def _deps(self, reads, writes):
        deps = []
        for k in reads:
            t = self.last_w.get(k)
            if t is not None:
                deps.append(t)
        for k in writes:
            t = self.last_w.get(k)
            if t is not None:
                deps.append(t)
            r = self.readers.get(k)
            if r:
                deps.extend(r.values())
        return deps

    def _record(self, tok, reads, writes):
        for k in reads:
            self.readers.setdefault(k, {})[tok[0].num] = tok
        for k in writes:
            self.last_w[k] = tok
            self.readers[k] = {}
        self.all_tokens[tok[0].num] = tok

    def op(self, e, fn, reads=(), writes=()):
        reads = self._expand(reads)
        writes = self._expand(writes)
        for tok in self._deps(reads, writes):
            if tok[2] == e and e in TRUST_SAME:
                continue
            self._wait(e, tok)
        ins = fn(self.eng[e])
        self.cnt[e] += 1
        ins.then_inc(self.sem[e], 1)
        tok = (self.sem[e], self.cnt[e], e)
        self._record(tok, reads, writes)
        return tok

    def dma(self, q, fns, reads=(), writes=()):
        reads = self._expand(reads)
        writes = self._expand(writes)
        half = NDMA // 2
        if q == 'sp':
            i = self.dma_rr_sp
            self.dma_rr_sp = (i + 1) % half
        else:
            i = half + self.dma_rr
            self.dma_rr = (self.dma_rr + 1) % half
        if self.dma_last[i] is not None:
            self._wait(q, self.dma_last[i])
        for tok in self._deps(reads, writes):
            self._wait(q, tok)
        for fn in fns:
            fn(self.eng[q]).then_inc(self.dma_sems[i], 16)
            self.dma_cnt[i] += 16
        tok = (self.dma_sems[i], self.dma_cnt[i], 'dma')
        self.dma_last[i] = tok
        self._record(tok, reads, writes)
        return tok

    def barrier(self):
        toks = list(self.all_tokens.values())
        for e in self.eng:
            for tok in toks:
                self._wait(e, tok)
        self.last_w = {}
        self.readers = {}

    def finish(self):
        for tok in list(self.all_tokens.values()):
            self._wait('sp', tok)


def _bf(x):
    return np.ascontiguousarray(x.astype(ml_dtypes.bfloat16))


def make_consts():
    c = {}
    ident = np.eye(128, dtype=np.float32)
    ones = np.ones((128, 128), np.float32)
    bd = np.zeros((128, 128), np.float32)
    bd[:64, :64] = 1.0 / 64
    bd[64:, 64:] = 1.0 / 64
    perm = np.zeros((128, 128), np.float32)
    for m in range(128):
        d = m % 64
        half = (d % 32) // 16
        partner = m + 16 if half == 0 else m - 16
        perm[partner, m] = 1.0
    c['cb'] = _bf(np.concatenate([ident, ones, bd, perm], axis=1))
    jf = np.zeros((128, 64), np.float32)
    jf[np.arange(64), 63 - np.arange(64)] = 1.0
    c['cf'] = np.ascontiguousarray(np.concatenate([ident, jf], axis=1))
    t = np.arange(1024)
    pos = np.stack([t // GRID_W, t % GRID_W], axis=0).astype(np.float32)
    inv = (1.0 / (10000.0 ** (np.arange(0, 32, 2, dtype=np.float32) / 32))).astype(np.float32)
    rope = np.zeros((128, 2, 1024), np.float32)
    for p in range(128):
        d = p % 64
        a = d // 32
        half = (d % 32) // 16
        i = d % 16
        ang = (pos[a] * inv[i]).astype(np.float32)
        rope[p, 0] = np.cos(ang)
        rope[p, 1] = -np.sin(ang) if half == 0 else np.sin(ang)
    c['rope'] = rope
    qc = np.arange(64)
    cs = np.clip(qc - 8, 0, 48)
    kc = np.arange(64)
    ok = (kc[:, None] >= cs[None, :]) & (kc[:, None] < cs[None, :] + 16)
    m = np.where(ok, 0.0, NEG).astype(np.float32)
    c['mask2'] = np.concatenate([m, m], axis=0)
    cc = np.arange(128)
    th = 2 * np.pi * np.outer(cc, cc) / 128
    c['f_cc'] = _bf(np.concatenate([np.cos(th), np.sin(th)], axis=1))
    for name, N in (('f_n0', 256), ('f_n1', 1024)):
        n = np.arange(N)
        thn = 2 * np.pi * (np.outer(n, n) % N) / N
        sc = 1.0 / np.sqrt(N * 128.0)
        cn = (np.cos(thn) * sc).reshape(N // 128, 128, N).transpose(1, 0, 2)
        sn = (-np.sin(thn) * sc).reshape(N // 128, 128, N).transpose(1, 0, 2)
        c[name] = _bf(np.stack([cn, sn], axis=1))
    N = 1024
    pm = np.zeros((128, 4, 5, 128), np.float32)
    for gi, w in enumerate((2, 4, 8, 16)):
        P = np.zeros((N, N), np.float64)
        for tt in range(N):
            lo = min(max(tt - w // 2, 0), N - 1)
            hi = min(max(tt + w // 2 - 1, 0), N - 1)
            P[lo:hi + 1, tt] = 1.0 / (hi - lo + 1)
            P[tt, tt] -= 1.0
        pm[:, gi, 0] = P[0:128, 128:256]
        pm[:, gi, 1] = P[0:128, 0:128]
        pm[:, gi, 2] = P[128:256, 128:256]
        pm[:, gi, 3] = P[896:1024, 896:1024]
        pm[:, gi, 4] = P[128:256, 0:128]
    c['pm'] = _bf(pm)
    return c


class Builder:
    def __init__(self, debug=None, depth=DEPTH, groups=(1, 0), stage=99, plan=None):
        self.stage = stage
        self.sub = 99
        self.plan = plan
        self.scopes = False
        self.ada_bg = None
        self.ada_tick_n = 0
        self.wmode = 'plain'
        self.issue_gidx = 0
        self.tile_gidx = {}
        self.rec_tiles = []
        self.rec_segs = []
        self.dma_issued = 0
        self.cast_issued = 0
        self.tile_fs = {}
        self.tile_bs = {}
        self.tile_nk = {}
        self.seg_i = 0
        self.debug = debug or []
        self.depth = depth
        self.groups = groups
        nc = bass.Bass("TRN2", target_bir_lowering=False)
        self.nc = nc
        self.tr = Tracker(nc)
        self.decl_dram()
        self.alloc()
        self.wtile_idx = 0
        self.cast_rr = 0

    def decl_dram(self):
        nc = self.nc
        di = lambda n, sh, dt=F32: nc.dram_tensor(n, list(sh), dt, kind="ExternalInput").ap()
        do = lambda n, sh, dt=F32: nc.dram_tensor(n, list(sh), dt, kind="ExternalOutput").ap()
        self.x_in = [di("x0", (512, D)), di("x1", (1024, D))]
        self.ck = di("ck", (DEPTH, NH, PAST, HD))
        self.cv = di("cv", (DEPTH, NH, PAST, HD))
        self.cond = di("cond", (2, D))
        self.norm1_g = di("norm1_g", (DEPTH, D))
        self.w_in = di("w_in", (DEPTH, D, IN_COLS))
        self.b_gate = di("b_gate", (DEPTH, 4 * D))
        self.q_norm_g = di("q_norm_g", (DEPTH, HD))
        self.k_norm_g = di("k_norm_g", (DEPTH, HD))
        self.rpb_pad = di("rpb_pad", (64 + DEPTH * NH * 15 * 31 + 64,))
        self.gmlp_norm_g = di("gmlp_norm_g", (DEPTH, BW))
        self.w_spatial = di("w_spatial", (DEPTH, 4, 128, 128))
        self.b_spatial = di("b_spatial", (DEPTH, 4 * 128))
        self.w_pool = di("w_pool", (DEPTH, 4, 128, 128))
        self.pool_scale = di("pool_scale", (DEPTH, BW))
        self.w_branch = di("w_branch", (DEPTH, D, D))
        self.w_out = di("w_out", (DEPTH, D, D))
        self.norm2_g = di("norm2_g", (DEPTH, D))
        self.w_mlp1 = di("w_mlp1", (DEPTH, D, DFF))
        self.w_mlp2 = di("w_mlp2", (DEPTH, DFF, D))
        self.w_ada = di("w_ada", (DEPTH, D, 6 * D))
        self.b_ada = di("b_ada", (DEPTH, 6 * D))
        self.c_cb = di("cb", (128, 512), BF16)
        self.c_cf = di("cf", (128, 192))
        self.c_rope = di("rope", (128, 2, 1024))
        self.c_mask2 = di("mask2", (128, 64))
        self.c_fcc = di("f_cc", (128, 256), BF16)
        self.c_fn = [di("f_n0", (128, 2, 2, 256), BF16), di("f_n1", (128, 2, 8, 1024), BF16)]
        self.c_pm = di("pm", (128, 4, 5, 128), BF16)
        self.y_out = [do("y0", (512, D)), do("y1", (1024, D))]
        self.nk = do("nk", (2, DEPTH, NH, 256, HD))
        self.nv = do("nv", (2, DEPTH, NH, 256, HD))
        self.wcache_l = [nc.dram_tensor(f"wcache{l}", [300, 128, NCH * 128], BF16, kind="Internal").ap() for l in range(DEPTH)]
        self.dbg = {}
        for name, shape in self.debug:
            self.dbg[name] = do("dbg_" + name, shape)

    def alloc(self):
        nc = self.nc
        S = nc.alloc_sbuf_tensor
        self.XT = S("XT", [128, NCH * 1024], F32)
        self.HT = S("HT", [128, NCH * 1024], BF16)
        self.WK = S("WK", [128, NCH * 1024], BF16)
        self.BR = S("BR", [128, NCH * 1024], BF16)
        self.WF = [S(f"WF{i}", [128, NCH, 128], F32) for i in range(2)]
        self.WB = [S(f"WB{i}", [128, NCH, 128], BF16) for i in range(3)]
        self.SM = [S(f"SM{i}", [128, 512], F32) for i in range(5)]
        self.CB = S("CB", [128, 512], BF16)
        self.IDFJ = S("IDFJ", [128, 192], F32)
        self.VT = S("VT", [128, 432], F32)
        self.VROW = S("VROW", [128, 128], F32)
        self.QKG = S("QKG", [128, 4], F32)
        self.SC = S("SC", [128, 2, 16], BF16)
        self.MV = S("MV", [128, DEPTH, 96, 2], F32)
        self.MOD = S("MOD", [128, DEPTH, 2, 2, 16], F32)
        self.REC = S("REC", [128, 16], F32)
        self.WSB = S("WSB", [128, 4, 128], BF16)
        self.WPB = S("WPB", [128, 4, 128], BF16)
        self.PS = [nc.alloc_psum_tensor(f"PS{i}", [128, 512], F32) for i in range(8)]
        self.IDF = self.IDFJ[:, 0:128]
        self.JF = self.IDFJ[0:64, 128:192]
        self.IDB = self.CB[:, 0:128]
        self.ONES = self.CB[:, 128:256]
        self.BD64 = self.CB[:, 256:384]
        self.PERM = self.CB[:, 384:512]

    def v3(self, t, T, dt=None):
        return t[:, 0:NCH * T].rearrange("p (c t) -> p c t", c=NCH)

    def psb(self, b):
        return self.PS[b][:].bitcast(BF16)

    def set_wslots(self, big):
        self.wf_slots = [(('wf', i), self.WF[i]) for i in range(2)]
        self.wb_slots = [(('wb', i), self.WB[i]) for i in range(3)]
        if big:
            for k in range(4):
                v = self.XT[:, 8192 + k * 2048: 8192 + (k + 1) * 2048].rearrange("p (k n) -> p k n", k=NCH)
                self.wf_slots.append((('wf', 2 + k), v))
                v = self.HT[:, 8192 + k * 2048: 8192 + (k + 1) * 2048].rearrange("p (k n) -> p k n", k=NCH)
                self.wb_slots.append((('wb', 3 + k), v))

    def end_segment(self):
        self.rec_segs.append(len(self.rec_tiles))
        self.seg_i += 1

    def _issue_dma(self, m, pieces, ada):
        nk = pieces[0].shape[0] // 128
        if self.wmode == 'read' and not ada:
            gidx = self.issue_gidx
            self.issue_gidx += 1
            bkey, wb = self.wb_slots[m % len(self.wb_slots)]
            src = self.wcache_l[gidx // 300][gidx % 300][:, 0:nk * 128].rearrange("p (k n) -> p k n", n=128)
            self.tr.dma('sp', [lambda e: e.dma_start(out=wb[:, 0:nk, :], in_=src)], writes=[bkey])
            self.tile_bs[m] = (bkey, wb)
            return
        key, wf = self.wf_slots[m % len(self.wf_slots)]
        fns = []
        col = 0
        for ap in pieces:
            n = ap.shape[1]
            src = ap.rearrange("(kc p) n -> p kc n", p=128)
            dst = wf[:, 0:nk, col:col + n]
            fns.append(lambda e, dst=dst, src=src: e.dma_start(out=dst, in_=src))
            col += n
        assert col == 128
        self.tr.dma('sp', fns, writes=[key])
        self.tile_fs[m] = (key, wf)
        self.tile_nk[m] = nk
        if self.wmode == 'write' and not ada:
            self.tile_gidx[m] = self.issue_gidx
            self.issue_gidx += 1

    def _issue_cast(self, m):
        tr = self.tr
        if m in self.tile_bs:
            return
        fkey, wf = self.tile_fs.pop(m)
        bkey, wb = self.wb_slots[m % len(self.wb_slots)]
        nk = self.tile_nk.pop(m)
        ce = 'dve' if (self.cast_rr % 2 == 0) else 'act'
        self.cast_rr += 1
        if ce == 'dve':
            tr.op('dve', lambda e: e.tensor_copy(wb[:, 0:nk, :], wf[:, 0:nk, :]), reads=[fkey], writes=[bkey])
        else:
            tr.op('act', lambda e: e.copy(wb[:, 0:nk, :], wf[:, 0:nk, :]), reads=[fkey], writes=[bkey])
        self.tile_bs[m] = (bkey, wb)
        if m in self.tile_gidx:
            gidx = self.tile_gidx.pop(m)
            dst = self.wcache_l[gidx // 300][gidx % 300][:, 0:nk * 128].rearrange("p (k n) -> p k n", n=128)
            tr.dma('pool', [lambda e: e.dma_start(out=dst, in_=wb[:, 0:nk, :])], reads=[bkey])

    def load_w(self, pieces, ada=False):
        n = self.wtile_idx
        self.wtile_idx += 1
        self.rec_tiles.append((pieces, ada))
        if self.plan is None:
            tiles, seg_end = None, n + 1
        else:
            tiles, segs = self.plan
            seg_end = segs[self.seg_i] if self.seg_i < len(segs) else len(tiles)
        depth = len(self.wb_slots) if self.wmode == 'read' else len(self.wf_slots)
        dma_to = min(n + depth - 1, seg_end - 1)
        cast_to = min(n + 1, seg_end - 1)
        while self.dma_issued <= max(dma_to, n):
            m = self.dma_issued
            p, a = (pieces, ada) if m == n else tiles[m]
            self._issue_dma(m, p, a)
            self.dma_issued += 1
        while self.cast_issued <= max(cast_to, n):
            self._issue_cast(self.cast_issued)
            self.cast_issued += 1
        return self.tile_bs.pop(n)

    def mm_group(self, out_ap, bs, rhs_fn, rkeys, bank, kcs=range(NCH), wcol=slice(0, 128)):
        bkey, wb = bs
        kcs = list(kcs)

        def fn(e):
            ins = None
            for n, kc in enumerate(kcs):
                ins = e.matmul(out_ap, wb[:, kc, wcol], rhs_fn(kc), start=(n == 0), stop=(n == len(kcs) - 1))
            return ins
        self.tr.op('pe', fn, reads=[bkey] + list(rkeys), writes=[('ps', bank)])

    def linear(self, piece_fn, n_out, in3, in_key, NT, consumer, lookahead=1):
        for oc in range(n_out):
            bs = self.load_w(piece_fn(oc))
            for tt in range(NT):
                bank = self.tr.bank()
                self.mm_group(self.PS[bank][:, :], bs,
                              lambda kc, tt=tt: in3[:, kc, tt * 512:(tt + 1) * 512],
                              [in_key(kc, tt) for kc in range(NCH)], bank)
                consumer(oc, tt, bank)
            self.ada_tick()

    def phase0(self, core_has_cond=True):
        nc, tr = self.nc, self.tr
        tr.dma('pool', [lambda e: e.dma_start(out=self.CB[:], in_=self.c_cb[:, :]),
                        lambda e: e.dma_start(out=self.IDFJ[:], in_=self.c_cf[:, :])],
               writes=['CB', 'IDF'])
        def rows(vec_ap, n):
            return vec_ap.rearrange("(n p) -> n p", p=128)
        for l in range(DEPTH):
            tr.dma('pool', [
                lambda e, l=l: e.dma_start(out=self.VROW[0:96, :], in_=rows(self.b_ada[l], 96)),
                lambda e, l=l: e.dma_start(out=self.VROW[96:112, :], in_=rows(self.norm1_g[l], 16)),
                lambda e, l=l: e.dma_start(out=self.VROW[112:128, :], in_=rows(self.norm2_g[l], 16)),
            ], writes=['VROW'])
            b = tr.bank()
            tr.op('pe', lambda e, b=b: e.transpose(self.PS[b][:, 0:128], self.VROW[:, :], self.IDF),
                  reads=['VROW', 'IDF'], writes=[('ps', b)])
            tr.op('dve', lambda e, b=b, l=l: e.tensor_copy(self.VT[:, 200 * l:200 * l + 128], self.PS[b][:, 0:128]),
                  reads=[('ps', b)], writes=['VT'])
            tr.dma('pool', [
                lambda e, l=l: e.dma_start(out=self.VROW[0:64, :], in_=rows(self.b_gate[l], 64)),
                lambda e, l=l: e.dma_start(out=self.VROW[64:68, :], in_=rows(self.gmlp_norm_g[l], 4)),
                lambda e, l=l: e.dma_start(out=self.VROW[68:72, :], in_=rows(self.pool_scale[l], 4)),
            ], writes=['VROW'])
            b = tr.bank()
            tr.op('pe', lambda e, b=b: e.transpose(self.PS[b][:, 0:72], self.VROW[0:72, :], self.IDFJ[0:72, 0:72]),
                  reads=['VROW', 'IDF'], writes=[('ps', b)])
            tr.op('dve', lambda e, b=b, l=l: e.tensor_copy(self.VT[:, 200 * l + 128:200 * l + 200], self.PS[b][:, 0:72]),
                  reads=[('ps', b)], writes=['VT'])
            fns = []
            for i, g in enumerate((self.q_norm_g, self.k_norm_g)):
                for hh in range(2):
                    fns.append(lambda e, g=g, l=l, i=i, hh=hh: e.dma_start(
                        out=self.QKG[hh * 64:(hh + 1) * 64, 2 * l + i:2 * l + i + 1],
                        in_=g[l].rearrange("(p o) -> p o", o=1)))
            tr.dma('pool', fns, writes=['QKG'])
        tr.dma('pool', [lambda e: e.dma_start(out=self.VROW[0:32, :],
                                               in_=self.cond.rearrange("i (n p) -> (i n) p", p=128))],
               writes=['VROW'])
        b = tr.bank()
        tr.op('pe', lambda e, b=b: e.transpose(self.PS[b][:, 0:32], self.VROW[0:32, :], self.IDFJ[0:32, 0:32]),
              reads=['VROW', 'IDF'], writes=[('ps', b)])
        tr.op('dve', lambda e, b=b: e.tensor_copy(self.VT[:, 400:432], self.PS[b][:, 0:32]),
              reads=[('ps', b)], writes=['VT'])
        tr.op('act', lambda e: e.activation(out=self.SM[0][:, 0:32], in_=self.VT[:, 400:432], func=AF.Sigmoid),
              reads=['VT'], writes=[('sm', 0)])
        tr.op('dve', lambda e: e.tensor_tensor(self.SC[:].rearrange("p i k -> p (i k)"), self.SM[0][:, 0:32],
                                               self.VT[:, 400:432], ALU.mult),
              reads=[('sm', 0), 'VT'], writes=['SC'])
        self.ada_state = {}
        self.ada_begin(0)
        while self.ada_step(0):
            pass
        self.ada_finish(0)

    def ada_begin(self, l):
        self.ada_state[l] = [self.tr.bank(hold=True), 0]

    def ada_step(self, l):
        st = self.ada_state.get(l)
        if st is None or st[1] >= 96:
            return False
        bank, oc = st
        bs = self.load_w([self.w_ada[l][:, oc * 128:(oc + 1) * 128]], ada=True)
        self.mm_group(self.PS[bank][:, 2 * oc:2 * oc + 2], bs, lambda kc: self.SC[:, :, kc], ['SC'], bank)
        st[1] += 1
        return True

    def ada_finish(self, l):
        tr = self.tr
        if l not in self.ada_state:
            return
        while self.ada_step(l):
            pass
        bank = self.ada_state.pop(l)[0]
        tr.op('dve', lambda e: e.tensor_tensor(
            self.MV[:, l, :, :], self.PS[bank][:, 0:192].rearrange("p (j i) -> p j i", i=2),
            self.VT[:, 200 * l:200 * l + 96].unsqueeze(2).broadcast_to([128, 96, 2]), ALU.add),
            reads=[('ps', bank), 'VT'], writes=[('MV', l)])
        tr.release(bank)
        for i in range(2):
            tr.op('dve', lambda e: e.scalar_tensor_tensor(
                self.MOD[:, l, i, 0, :], self.MV[:, l, 16:32, i], 1.0, self.VT[:, 200 * l + 96:200 * l + 112],
                ALU.add, ALU.mult), reads=[('MV', l), 'VT'], writes=[('MOD', l)])
            tr.op('dve', lambda e: e.scalar_tensor_tensor(
                self.MOD[:, l, i, 1, :], self.MV[:, l, 64:80, i], 1.0, self.VT[:, 200 * l + 112:200 * l + 128],
                ALU.add, ALU.mult), reads=[('MV', l), 'VT'], writes=[('MOD', l)])

    def ada_tick(self, n=1):
        if self.ada_bg is not None:
            for _ in range(n):
                self.ada_step(self.ada_bg)

    def mvcol(self, l, j, i):
        return self.MV[:, l, j, i:i + 1]

    def load_x(self, g, T):
        tr = self.tr
        X3 = self.v3(self.XT, T)
        stg = [self.WK[:, k * 4096:(k + 1) * 4096].bitcast(F32) for k in range(2)]
        for tb in range(T // 128):
            s = tb % 2
            tr.dma('pool', [lambda e, tb=tb, s=s: e.dma_start(out=stg[s], in_=self.x_in[g][tb * 128:(tb + 1) * 128, :])],
                   writes=[('stg', s)])
            for c0 in range(0, NCH, 4):
                b = tr.bank()

                def fn(e, c0=c0, b=b, s=s):
                    ins = None
                    for q in range(4):
                        ins = e.transpose(self.PS[b][:, q * 128:(q + 1) * 128],
                                          stg[s][:, (c0 + q) * 128:(c0 + q + 1) * 128], self.IDF)
                    return ins
                tr.op('pe', fn, reads=[('stg', s), 'IDF'], writes=[('ps', b)])
                eng = 'dve' if (c0 // 4) % 2 == 0 else 'act'
                outap = X3[:, c0:c0 + 4, tb * 128:(tb + 1) * 128]
                inap = self.PS[b][:, :].rearrange("p (q t) -> p q t", q=4)
                keys = [('x', c, tb // 4) for c in range(c0, c0 + 4)]
                if eng == 'dve':
                    tr.op('dve', lambda e, o=outap, i=inap: e.tensor_copy(o, i), reads=[('ps', b)], writes=keys)
                else:
                    tr.op('act', lambda e, o=outap, i=inap: e.copy(o, i), reads=[('ps', b)], writes=keys)

    def store_y(self, g, T):
        tr = self.tr
        X3 = self.v3(self.XT, T)
        stg = [self.WK[:, k * 4096:(k + 1) * 4096].bitcast(F32) for k in range(2)]
        for tb in range(T // 128):
            s = tb % 2
            for c0 in range(0, NCH, 4):
                b = tr.bank()

                def fn(e, c0=c0, b=b, tb=tb):
                    ins = None
                    for q in range(4):
                        ins = e.transpose(self.PS[b][:, q * 128:(q + 1) * 128],
                                          X3[:, c0 + q, tb * 128:(tb + 1) * 128], self.IDF)
                    return ins
                tr.op('pe', fn, reads=[('x', c, tb // 4) for c in range(c0, c0 + 4)] + ['IDF'], writes=[('ps', b)])
                eng = 'dve' if (c0 // 4) % 2 == 0 else 'act'
                outap = stg[s][:, c0 * 128:(c0 + 4) * 128]
                if eng == 'dve':
                    tr.op('dve', lambda e, o=outap, b=b: e.tensor_copy(o, self.PS[b][:, :]),
                          reads=[('ps', b)], writes=[('stg', s, c0)])
                else:
                    tr.op('act', lambda e, o=outap, b=b: e.copy(o, self.PS[b][:, :]),
                          reads=[('ps', b)], writes=[('stg', s, c0)])
            tr.dma('pool', [lambda e, tb=tb, s=s: e.dma_start(out=self.y_out[g][tb * 128:(tb + 1) * 128, :], in_=stg[s])],
                   reads=[('stg', s, c0) for c0 in range(0, NCH, 4)])

    def rstd_from_bank(self, bank, scale, R):
        raise NotImplementedError

    def norm_mod(self, l, i, which, T):
        tr = self.tr
        NT = T // 512
        X3 = self.v3(self.XT, T)
        H3 = self.v3(self.HT, T)
        SQ = self.SM[0][:].bitcast(BF16)
        R = self.SM[1]
        for tt in range(NT):
            ts = slice(tt * 512, (tt + 1) * 512)
            bank = tr.bank()
            for c in range(NCH):
                sq = SQ[:, (c % 2) * 512:(c % 2 + 1) * 512]
                tr.op('act', lambda e, sq=sq, c=c: e.activation(out=sq, in_=X3[:, c, ts], func=AF.Square),
                      reads=[('x', c, tt)], writes=[('smh', 0, c % 2)])
                tr.op('pe', lambda e, sq=sq, c=c: e.matmul(self.PS[bank][:, :], self.ONES, sq,
                                                          start=(c == 0), stop=(c == NCH - 1)),
                      reads=[('smh', 0, c % 2), 'CB'], writes=[('ps', bank)])
            tr.op('dve', lambda e: e.tensor_scalar(R[:], self.PS[bank][:, :], 1.0 / D, EPS, ALU.mult, ALU.add),
                  reads=[('ps', bank)], writes=[('sm', 1)])
            tr.op('act', lambda e: e.activation(out=R[:], in_=R[:], func=AF.Sqrt), reads=[('sm', 1)], writes=[('sm', 1)])
            tr.op('dve', lambda e: e.reciprocal(R[:], R[:]), reads=[('sm', 1)], writes=[('sm', 1)])
            for c in range(NCH):
                tmp = self.SM[2 + (c % 2)]
                a_col = self.MOD[:, l, i, which, c:c + 1]
                b_col = self.mvcol(l, (0 if which == 0 else 48) + c, i)
                tr.op('dve', lambda e, tmp=tmp, c=c, a_col=a_col: e.scalar_tensor_tensor(
                    tmp[:], X3[:, c, ts], a_col, R[:], ALU.mult, ALU.mult),
                    reads=[('x', c, tt), ('sm', 1), ('MOD', l)], writes=[('sm', 2 + c % 2)])
                tr.op('act', lambda e, tmp=tmp, c=c, b_col=b_col: e.activation(
                    out=H3[:, c, ts], in_=tmp[:], func=AF.Identity, bias=b_col, scale=1.0),
                    reads=[('sm', 2 + c % 2), ('MV', l)], writes=[('h', c, tt)])

    def rsqrt_inplace(self, R, key):
        tr = self.tr
        tr.op('act', lambda e: e.activation(out=R, in_=R, func=AF.Sqrt), reads=[key], writes=[key])
        tr.op('dve', lambda e: e.reciprocal(R, R), reads=[key], writes=[key])

    def w_in_piece(self, l, col0):
        return lambda oc: [self.w_in[l][:, col0 + oc * 128: col0 + (oc + 1) * 128]]

    def attention(self, g, l, T):
        tr = self.tr
        NT = T // 512
        NTB = T // 128
        H3 = self.v3(self.HT, T)
        W3 = self.v3(self.WK, T)
        B3 = self.v3(self.BR, T)
        hkey = lambda kc, tt: ('h', kc, tt)
        base = 4 * T
        nvb = NTB
        VL = self.BR[:, base: base + nvb * NH * 65].rearrange("p (b h d) -> p b h d", b=nvb, h=NH)
        off = base + nvb * NH * 65
        brkeys = [('br', c, tt) for c in range(4, NCH) for tt in range(NT)]
        if g == 1:
            VC = self.BR[:, off: off + 4 * NH * 65].rearrange("p (b h d) -> p b h d", b=4, h=NH)
            off += 4 * NH * 65
            KC = self.BR[:, off: off + 4 * 512].rearrange("p (j s) -> p j s", j=4)
            off += 4 * 512
            off += off % 2
            TAB = self.BR[:, off: off + 2 * 19 * 64].bitcast(F32).rearrange("p (e q) -> p e q", e=19)
            off += 2 * 19 * 64
            assert off <= NCH * T, off
        tr.op('dve', lambda e: e.tensor_copy(VL[:, :, :, 64:65], self.CB[:, 128:128 + nvb * NH].rearrange('p (b h d) -> p b h d', b=nvb, h=NH)),
              reads=['CB'], writes=brkeys + [('vl', 0), ('vl', 1)])
        if g == 1:
            tr.op('dve', lambda e: e.tensor_copy(VC[:, :, :, 64:65], self.CB[:, 128:128 + 4 * NH].rearrange('p (b h d) -> p b h d', b=4, h=NH)),
                  reads=['CB'], writes=brkeys + [('vc', i) for i in range(4)])

        def v_consumer(oc, tt, bank):
            vf = self.SM[2 + (oc % 2)]
            tr.op('dve', lambda e: e.tensor_copy(vf[:], self.PS[bank][:, :]), reads=[('ps', bank)],
                  writes=[('sm', 2 + oc % 2)])
            if self.sub < 0.4:
                return
            b2 = tr.bank()

            def fn(e):
                ins = None
                for q in range(4):
                    ins = e.transpose(self.PS[b2][:, q * 128:(q + 1) * 128], vf[:, q * 128:(q + 1) * 128], self.IDF)
                return ins
            tr.op('pe', fn, reads=[('sm', 2 + oc % 2), 'IDF'], writes=[('ps', b2)])
            if self.sub < 0.6:
                return
            src = self.PS[b2][:, :].rearrange("p (q h d) -> p q h d", q=4, h=2)
            tr.op('act', lambda e: e.copy(VL[:, tt * 4:(tt + 1) * 4, 2 * oc:2 * oc + 2, 0:64], src),
                  reads=[('ps', b2)], writes=[('vl', tt), ('lock', b2)])
            if self.sub < 0.8:
                return
            if g == 0:
                dst = self.cst[1][:, :, oc * 128:(oc + 1) * 128]
                tr.op('dve', lambda e: e.tensor_copy(dst, self.PS[b2][:, :].rearrange("p (q f) -> p q f", q=4)),
                      reads=[('ps', b2), ('lock', b2)], writes=[('cstv', oc)])
        if g == 0:
            self.cst = [self.WK[:, 8192 + k * 4096: 8192 + (k + 1) * 4096].bitcast(F32).rearrange(
                "p (q f) -> p q f", q=4) for k in range(2)]
        self.linear(self.w_in_piece(l, 2 * BW), 4, H3, hkey, NT, v_consumer)
        if self.sub < 2:
            return
        if g == 0:
            self.store_cache(self.nv, l, 1, [('cstv', oc) for oc in range(4)])
        if self.sub < 3:
            return

        if g == 1:
            self.load_ctx(l, VC, KC)
            rope = self.WK[:, 8 * T: 8 * T + 4096].bitcast(F32).rearrange("p (a t) -> p a t", a=2)
            ropekeys = [('wk', c, tt) for c in range(8, 12) for tt in range(NT)]
            tr.dma('pool', [lambda e: e.dma_start(out=rope, in_=self.c_rope[:, :, :])], writes=ropekeys)

        def qk_consumer(which):
            gcol = self.QKG[:, 2 * l + which:2 * l + which + 1]

            def consumer(oc, tt, bank):
                ts = slice(tt * 512, (tt + 1) * 512)
                sq = self.SM[0][:].bitcast(BF16)[:, 0:512]
                qf = self.SM[2 + (oc % 2)]
                R = self.SM[1]
                tr.op('dve', lambda e: e.tensor_copy(qf[:], self.PS[bank][:, :]), reads=[('ps', bank)],
                      writes=[('sm', 2 + oc % 2)])
                tr.op('act', lambda e: e.activation(out=sq, in_=qf[:], func=AF.Square),
                      reads=[('sm', 2 + oc % 2)], writes=[('smh', 0, 0)])
                b2 = tr.bank()
                tr.op('pe', lambda e: e.matmul(self.PS[b2][:, :], self.BD64, sq, start=True, stop=True),
                      reads=[('smh', 0, 0), 'CB'], writes=[('ps', b2)])
                tr.op('dve', lambda e: e.tensor_scalar(R[:], self.PS[b2][:, :], EPS, None, ALU.add),
                      reads=[('ps', b2)], writes=[('sm', 1)])
                self.rsqrt_inplace(R[:], ('sm', 1))
                tr.op('dve', lambda e: e.scalar_tensor_tensor(qf[:], qf[:], gcol, R[:], ALU.mult, ALU.mult),
                      reads=[('sm', 2 + oc % 2), ('sm', 1), 'QKG'], writes=[('sm', 2 + oc % 2)])
                wkc = (0 if which == 0 else 4) + oc
                tr.op('act', lambda e: e.copy(W3[:, wkc, ts], qf[:]), reads=[('sm', 2 + oc % 2)],
                      writes=[('wk', wkc, tt)])
                if g == 0 and which == 1:
                    b3 = tr.bank()

                    def fn(e):
                        ins = None
                        for q in range(4):
                            ins = e.transpose(self.PS[b3][:, q * 128:(q + 1) * 128], qf[:, q * 128:(q + 1) * 128],
                                              self.IDF)
                        return ins
                    tr.op('pe', fn, reads=[('sm', 2 + oc % 2), 'IDF'], writes=[('ps', b3)])
                    dst = self.cst[0][:, :, oc * 128:(oc + 1) * 128]
                    tr.op('dve', lambda e: e.tensor_copy(dst, self.PS[b3][:, :].rearrange("p (q f) -> p q f", q=4)),
                          reads=[('ps', b3)], writes=[('cstk', oc)])
                if g == 1:
                    b3 = tr.bank()
                    tr.op('pe', lambda e: e.matmul(self.PS[b3][:, :], self.PERM, W3[:, wkc, ts], start=True, stop=True),
                          reads=[('wk', wkc, tt), 'CB'], writes=[('ps', b3)])
                    t2 = self.SM[4]
                    tr.op('dve', lambda e: e.tensor_tensor(t2[:], self.PS[b3][:, :], rope[:, 1, ts], ALU.mult),
                          reads=[('ps', b3)] + ropekeys, writes=[('sm', 4)])
                    tr.op('dve', lambda e: e.tensor_tensor(qf[:], qf[:], rope[:, 0, ts], ALU.mult),
                          reads=[('sm', 2 + oc % 2)] + ropekeys, writes=[('sm', 2 + oc % 2)])
                    dchunk = (12 if which == 0 else 4) + oc
                    tr.op('dve', lambda e: e.tensor_tensor(W3[:, dchunk, ts], qf[:], t2[:], ALU.add),
                          reads=[('sm', 2 + oc % 2), ('sm', 4)], writes=[('wk', dchunk, tt)])
            return consumer
        self.linear(self.w_in_piece(l, 0), 4, H3, hkey, NT, qk_consumer(0))
        self.linear(self.w_in_piece(l, BW), 4, H3, hkey, NT, qk_consumer(1))
        if self.sub < 4:
            return
        if g == 0:
            self.store_cache(self.nk, l, 0, [('cstk', oc) for oc in range(4)])
        if self.sub < 5:
            return

        if g == 0:
            self.attn_ctx(T, W3, B3, VL)
        else:
            self.attn_latent(l, T, W3, B3, VL, VC, KC, TAB)

    def store_cache(self, dst, l, which, keys):
        tr = self.tr
        fns = []
        for tb in range(4):
            b, s0 = tb // 2, (tb % 2) * 128
            d_ap = dst[b, l].rearrange("h s d -> s h d")[s0:s0 + 128]
            s_ap = self.cst[which][:, tb, :].rearrange("p (h d) -> p h d", h=NH)
            fns.append(lambda e, d_ap=d_ap, s_ap=s_ap: e.dma_start(out=d_ap, in_=s_ap))
        tr.dma('pool', fns, reads=keys)

    def load_ctx(self, l, VC, KC):
        tr = self.tr
        for blk in range(4):
            st = self.SM[2 + (blk % 2)]
            tr.dma('pool', [lambda e, blk=blk, st=st: e.dma_start(
                out=st[:].rearrange("p (h d) -> p h d", h=NH),
                in_=self.cv[l].rearrange("h s d -> s h d")[blk * 128:(blk + 1) * 128])],
                writes=[('sm', 2 + blk % 2)])
            tr.op('act', lambda e, blk=blk, st=st: e.copy(VC[:, blk, :, 0:64], st[:].rearrange("p (h d) -> p h d", h=NH)),
                  reads=[('sm', 2 + blk % 2)], writes=[('vc', blk)])
        for blk in range(4):
            st = self.SM[2 + (blk % 2)]
            tr.dma('pool', [lambda e, blk=blk, st=st: e.dma_start(
                out=st[:].rearrange("p (h d) -> p h d", h=NH),
                in_=self.ck[l].rearrange("h s d -> s h d")[blk * 128:(blk + 1) * 128])],
                writes=[('sm', 2 + blk % 2)])
            b = tr.bank()

            def fn(e, b=b, st=st):
                ins = None
                for j in range(4):
                    ins = e.transpose(self.PS[b][:, j * 128:(j + 1) * 128], st[:, j * 128:(j + 1) * 128], self.IDF)
                return ins
            tr.op('pe', fn, reads=[('sm', 2 + blk % 2), 'IDF'], writes=[('ps', b)])
            tr.op('act', lambda e, b=b, blk=blk: e.copy(KC[:, :, blk * 128:(blk + 1) * 128],
                                                        self.PS[b][:, :].rearrange("p (j s) -> p j s", j=4)),
                  reads=[('ps', b)], writes=[('kc', blk)])

    def attn_finish(self, bo, nh, hbase, atok, atok_key, rkeys_extra=()):
        tr = self.tr
        o3 = self.PS[bo][:, 0:nh * 65].rearrange("p (h d) -> p h d", h=nh)
        rec = self.REC[:, 0:nh]
        tr.op('dve', lambda e: e.reciprocal(rec, o3[:, :, 64]), reads=[('ps', bo)], writes=['REC'])
        out = atok[:, hbase * 64:(hbase + nh) * 64].rearrange("p (h d) -> p h d", h=nh)
        tr.op('dve', lambda e: e.tensor_tensor(out, o3[:, :, 0:64], rec.unsqueeze(2).broadcast_to([128, nh, 64]), ALU.mult),
              reads=[('ps', bo), 'REC'], writes=[atok_key])

    def attn_ctx(self, T, W3, B3, VL):
        tr = self.tr
        PT = [self.WK[:, 8 * T + k * 256: 8 * T + (k + 1) * 256].rearrange("p (b q) -> p b q", b=2) for k in range(4)]
        ptkeys = [('wk', c, 0) for c in range(8, 12)]
        pti = 0
        for s in range(2):
            for qb in range(2):
                q0 = s * 256 + qb * 128
                atok = self.SM[4][:].bitcast(BF16)[:, 0:512]
                bos = [tr.bank(hold=True), tr.bank(hold=True)]
                for h in range(NH):
                    j, pb = h // 2, (h % 2) * 64
                    bl = tr.bank()

                    def fn(e, bl=bl, j=j, pb=pb):
                        ins = None
                        for kb in range(2):
                            ins = e.matmul(self.PS[bl][:, kb * 128:(kb + 1) * 128],
                                           W3[pb:pb + 64, 4 + j, s * 256 + kb * 128: s * 256 + (kb + 1) * 128],
                                           W3[pb:pb + 64, j, q0:q0 + 128], start=True, stop=True)
                        return ins
                    tr.op('pe', fn, reads=[('wk', 4 + j, 0), ('wk', j, 0)], writes=[('ps', bl)])
                    pt = PT[pti % 4]
                    ptk = ('pt', pti % 4)
                    pti += 1
                    tr.op('act', lambda e, pt=pt, bl=bl: e.activation(
                        out=pt, in_=self.PS[bl][:, 0:256].rearrange("p (b q) -> p b q", b=2), func=AF.Exp, scale=0.125),
                        reads=[('ps', bl)], writes=[ptk] + ptkeys)
                    bo = bos[h // 4]
                    oreg = self.PS[bo][:, (h % 4) * 65:(h % 4 + 1) * 65]

                    def fn2(e, pt=pt, oreg=oreg, h=h):
                        ins = None
                        for kb in range(2):
                            ins = e.matmul(oreg, pt[:, kb, :], VL[:, s * 2 + kb, h, :], start=(kb == 0), stop=(kb == 1))
                        return ins
                    tr.op('pe', fn2, reads=[ptk, ('vl', 0)], writes=[('ps', bo)])
                    if h % 4 == 3:
                        self.attn_finish(bo, 4, h - 3, atok, ('sm', 4))
                        tr.release(bo)
                self.atok_to_br(atok, ('sm', 4), B3, q0, 0, 4)

    def atok_to_br(self, atok, key, B3, q0, c0, nchunk):
        tr = self.tr
        b = tr.bank()
        pb = self.psb(b)

        def fn(e):
            ins = None
            for jj in range(nchunk):
                ins = e.transpose(pb[:, jj * 128:(jj + 1) * 128], atok[:, jj * 128:(jj + 1) * 128], self.IDB)
            return ins
        tr.op('pe', fn, reads=[key, 'CB'], writes=[('ps', b)])
        tr.op('act', lambda e: e.copy(B3[:, c0:c0 + nchunk, q0:q0 + 128],
                                      pb[:, 0:nchunk * 128].rearrange("p (j t) -> p j t", j=nchunk)),
              reads=[('ps', b)], writes=[('br', c, q0 // 512) for c in range(c0, c0 + nchunk)])

    def build_bias(self, l, h, TAB):
        tr = self.tr
        TT = self.SM[0]
        base = 64 + (l * NH + h) * 15 * 31 - 48
        tt_a = self.SM[0][0:64, :].rearrange("p (e k) -> p e k", e=8)
        tt_b = self.SM[1][0:64, :].rearrange("p (e k) -> p e k", e=8)
        src_a = bass.AP(self.rpb_pad.tensor, base, [[1, 64], [31, 8], [1, 64]])
        src_b = bass.AP(self.rpb_pad.tensor, base + 7 * 31, [[1, 64], [31, 8], [1, 64]])
        tr.dma('pool', [lambda e: e.dma_start(out=tt_a, in_=src_a), lambda e: e.dma_start(out=tt_b, in_=src_b)],
               writes=[('sm', 0), ('sm', 1)])
        tabkeys = ['TAB']
        for half, ttx in ((0, self.SM[0]), (1, self.SM[1])):
            b = tr.bank()

            def fn(e, b=b, ttx=ttx):
                ins = None
                for k in range(7):
                    ins = e.matmul(self.PS[b][:, k * 64:(k + 1) * 64], ttx[0:64, k * 64:(k + 2) * 64], self.JF,
                                   start=True, stop=True)
                return ins
            tr.op('pe', fn, reads=[('sm', half), 'IDF'], writes=[('ps', b)])
            tr.op('dve', lambda e, b=b, half=half: e.tensor_tensor(
                TAB[:, 7 * half:7 * half + 7, :], self.PS[b][:, 0:448].rearrange("p (e q) -> p e q", e=7),
                self.MASK2[:, :].unsqueeze(1).broadcast_to([128, 7, 64]), ALU.add),
                reads=[('ps', b), 'MASK2'], writes=tabkeys)
        tr.op('dve', lambda e: e.tensor_copy(TAB[:, 15:18, :], TAB[:, 4:10:2, :]), reads=tabkeys, writes=tabkeys)
        tr.op('dve', lambda e: e.tensor_copy(TAB[:, 14, :], TAB[:, 2, :]), reads=tabkeys, writes=tabkeys)
        tr.op('dve', lambda e: e.tensor_copy(TAB[:, 18, :], TAB[:, 10, :]), reads=tabkeys, writes=tabkeys)
        tr.op('dve', lambda e: e.memset(TAB[0:64, 14, :], NEG), reads=tabkeys, writes=tabkeys)
        tr.op('dve', lambda e: e.memset(TAB[64:128, 18, :], NEG), reads=tabkeys, writes=tabkeys)

    def attn_latent(self, l, T, W3, B3, VL, VC, KC, TAB):
        tr = self.tr
        rows = 16
        o = 8 * T
        PTE, PTO, PTC = [], [], []
        for k in range(2):
            PTE.append(self.WK[:, o:o + 640].rearrange("p (b q) -> p b q", b=5)); o += 640
            PTO.append(self.WK[:, o:o + 640].rearrange("p (b q) -> p b q", b=5)); o += 640
            PTC.append(self.WK[:, o:o + 512].rearrange("p (b q) -> p b q", b=4)); o += 512
        assert o <= 12 * T
        ptkeys = [('wk', c, tt) for c in range(8, 12) for tt in range(2)]
        tr.op('pool', lambda e: e.memset(self.WK[:, 8 * T: 8 * T + 2 * (640 + 640 + 512)], 0.0), writes=ptkeys)
        MASKK = 'MASK2'
        tmps = [self.SM[2], self.SM[3]]
        ti = 0
        for j in range(4):
            atokp = self.SM[4][:].bitcast(BF16).rearrange("p (r f) -> p r f", r=8)
            for hh in range(2):
                h = 2 * j + hh
                pb = hh * 64
                self.build_bias(l, h, TAB)
                for rp4 in range(2):
                    bo = tr.bank(hold=True)
                    pending = None
                    for rpi in range(4):
                        rp = rp4 * 4 + rpi
                        oreg = self.PS[bo][:, rpi * 65:(rpi + 1) * 65]
                        k2 = (rp % 2)
                        bc = tr.bank()

                        def fnc(e, bc=bc, rp=rp):
                            ins = None
                            for i in range(4):
                                ins = e.matmul(self.PS[bc][:, i * 128:(i + 1) * 128], KC[pb:pb + 64, j, i * 128:(i + 1) * 128],
                                               W3[pb:pb + 64, j, rp * 128:(rp + 1) * 128], start=True, stop=True)
                            return ins
                        tr.op('pe', fnc, reads=[('kc', i) for i in range(4)] + [('wk', j, rp // 4)], writes=[('ps', bc)])
                        rowinfo = []
                        for rr in range(2):
                            r = 2 * rp + rr
                            rs = min(max(r - 4, 0), rows - 8)
                            if rs % 2 == 0:
                                nb, b0 = 4, rs // 2
                                e0 = rs - r + 7
                                bias = TAB[:, e0:e0 + 7:2, :]
                            else:
                                nb, b0 = 5, (rs - 1) // 2
                                bias = TAB[:, 14:19, :]
                            bl = tr.bank()

                            def fnl(e, bl=bl, nb=nb, b0=b0, r=r):
                                ins = None
                                for i in range(nb):
                                    ins = e.matmul(self.PS[bl][:, i * 64:(i + 1) * 64],
                                                   W3[pb:pb + 64, 4 + j, (b0 + i) * 128:(b0 + i + 1) * 128],
                                                   W3[pb:pb + 64, 12 + j, r * 64:(r + 1) * 64], start=True, stop=True)
                                return ins
                            tr.op('pe', fnl, reads=[('wk', 4 + j, 0), ('wk', 4 + j, 1), ('wk', 12 + j, r // 8)],
                                  writes=[('ps', bl)])
                            rowinfo.append((rr, nb, b0, bias, bl))
                        if pending is not None:
                            pending()
                        ptc = PTC[k2]
                        tr.op('act', lambda e, ptc=ptc, bc=bc: e.activation(
                            out=ptc, in_=self.PS[bc][:, :].rearrange("p (b q) -> p b q", b=4), func=AF.Exp, scale=0.125),
                            reads=[('ps', bc)], writes=[('ptc', k2)] + ptkeys)
                        pv = [(ptc[:, i, :], VC[:, i, h, :]) for i in range(4)]
                        pvkeys = [('ptc', k2)] + [('vc', i) for i in range(4)]
                        for (rr, nb, b0, bias, bl) in rowinfo:
                            tmp = tmps[ti % 2]
                            tk = ('sm', 2 + ti % 2)
                            ti += 1
                            tv = tmp[:, 0:nb * 64].rearrange("p (b q) -> p b q", b=nb)
                            tr.op('dve', lambda e, tv=tv, bl=bl, nb=nb, bias=bias: e.scalar_tensor_tensor(
                                tv, self.PS[bl][:, 0:nb * 64].rearrange("p (b q) -> p b q", b=nb), 0.125, bias,
                                ALU.mult, ALU.add), reads=[('ps', bl), 'TAB'], writes=[tk])
                            ptl = (PTE if rr == 0 else PTO)[k2]
                            plk = ('ptl', rr, k2)
                            tr.op('act', lambda e, ptl=ptl, tv=tv, nb=nb, rr=rr: e.activation(
                                out=ptl[:, 0:nb, rr * 64:(rr + 1) * 64], in_=tv, func=AF.Exp),
                                reads=[tk], writes=[plk] + ptkeys)
                            pv += [(ptl[:, i, :], VL[:, b0 + i, h, :]) for i in range(nb)]
                            pvkeys += [plk, ('vl', 0), ('vl', 1)]

                        def do_pv(pv=pv, oreg=oreg, pvkeys=pvkeys, bo=bo):
                            def fnpv(e):
                                ins = None
                                for n, (lt, rh) in enumerate(pv):
                                    ins = e.matmul(oreg, lt, rh, start=(n == 0), stop=(n == len(pv) - 1))
                                return ins
                            tr.op('pe', fnpv, reads=pvkeys, writes=[('ps', bo)])
                        pending = do_pv
                    pending()
                    o3 = self.PS[bo][:, 0:260].rearrange("p (r d) -> p r d", r=4)
                    rec = self.REC[:, 0:4]
                    tr.op('dve', lambda e, o3=o3, rec=rec: e.reciprocal(rec, o3[:, :, 64]), reads=[('ps', bo)], writes=['REC'])
                    outv = atokp[:, rp4 * 4:(rp4 + 1) * 4, hh * 64:(hh + 1) * 64]
                    tr.op('dve', lambda e, o3=o3, rec=rec, outv=outv: e.tensor_tensor(
                        outv, o3[:, :, 0:64], rec.unsqueeze(2).broadcast_to([128, 4, 64]), ALU.mult),
                        reads=[('ps', bo), 'REC'], writes=[('sm', 4)])
                    tr.release(bo)
                    self.ada_tick(2)
            for half in range(2):
                b = tr.bank()
                pbv = self.psb(b)

                def fnt(e, pbv=pbv, half=half):
                    ins = None
                    for q in range(4):
                        ins = e.transpose(pbv[:, q * 128:(q + 1) * 128], atokp[:, half * 4 + q, :], self.IDB)
                    return ins
                tr.op('pe', fnt, reads=[('sm', 4), 'CB'], writes=[('ps', b)])
                tr.op('act', lambda e, pbv=pbv, half=half: e.copy(B3[:, j, half * 512:(half + 1) * 512], pbv[:, 0:512]),
                      reads=[('ps', b)], writes=[('br', j, half)])

    def gelu_from_bank(self, bank, out_ap, out_key, si):
        tr = self.tr
        X = self.SM[2 + si]
        Tm = self.SM[4] if si == 0 else self.SM[1]
        xk, tk = ('sm', 2 + si), (('sm', 4) if si == 0 else ('sm', 1))
        tr.op('act', lambda e: e.copy(X[:], self.PS[bank][:, :]), reads=[('ps', bank)], writes=[xk])
        tr.op('dve', lambda e: e.tensor_tensor(Tm[:], X[:], X[:], ALU.mult), reads=[xk], writes=[tk])
        tr.op('dve', lambda e: e.tensor_scalar(Tm[:], Tm[:], 0.044715, 1.0, ALU.mult, ALU.add), reads=[tk], writes=[tk])
        tr.op('dve', lambda e: e.tensor_tensor(Tm[:], Tm[:], X[:], ALU.mult), reads=[tk, xk], writes=[tk])
        tr.op('act', lambda e: e.activation(out=Tm[:], in_=Tm[:], func=AF.Sigmoid, scale=GELU_C), reads=[tk], writes=[tk])
        tr.op('dve', lambda e: e.tensor_tensor(out_ap, Tm[:], X[:], ALU.mult), reads=[tk, xk], writes=[out_key])

    def gmlp(self, g, l, T):
        tr = self.tr
        NT = T // 512
        H3 = self.v3(self.HT, T)
        W3 = self.v3(self.WK, T)
        B3 = self.v3(self.BR, T)
        hkey = lambda kc, tt: ('h', kc, tt)
        st = self.SM[0]
        tr.dma('pool', [lambda e: e.dma_start(out=st[:].rearrange("p (g q) -> p g q", g=4),
                                               in_=self.w_spatial[l].rearrange("g p q -> p g q"))], writes=[('sm', 0)])
        b = tr.bank()

        def fn(e):
            ins = None
            for gg in range(4):
                ins = e.transpose(self.PS[b][:, gg * 128:(gg + 1) * 128], st[:, gg * 128:(gg + 1) * 128], self.IDF)
            return ins
        tr.op('pe', fn, reads=[('sm', 0), 'IDF'], writes=[('ps', b)])
        tr.op('act', lambda e: e.copy(self.WSB[:].rearrange("p g q -> p (g q)"), self.PS[b][:, :]),
              reads=[('ps', b)], writes=['WSB'])
        BSB = self.SM[0]
        tr.dma('pool', [lambda e: e.dma_start(out=BSB[:], in_=bass.AP(self.b_spatial.tensor, l * 512, [[0, 128], [1, 512]]))],
               writes=[('sm', 0)])
        cnt = [0]

        def u_consumer(oc, tt, bank):
            ts = slice(tt * 512, (tt + 1) * 512)
            self.gelu_from_bank(bank, W3[:, oc, ts], ('wk', oc, tt), cnt[0] % 2)
            cnt[0] += 1
        self.linear(self.w_in_piece(l, 3 * BW), 4, H3, hkey, NT, u_consumer)
        self.linear(self.w_in_piece(l, 4 * BW), 4, H3, hkey, NT,
                    lambda oc, tt, bank: u_consumer(4 + oc, tt, bank))
        for tt in range(NT):
            ts = slice(tt * 512, (tt + 1) * 512)
            bank = tr.bank()
            SQ = self.SM[2][:].bitcast(BF16)
            for c in range(4):
                sq = SQ[:, (c % 2) * 512:(c % 2 + 1) * 512]
                tr.op('dve', lambda e, sq=sq, c=c: e.tensor_tensor(sq, W3[:, 4 + c, ts], W3[:, 4 + c, ts], ALU.mult),
                      reads=[('wk', 4 + c, tt)], writes=[('smh', 2, c % 2)])
                tr.op('pe', lambda e, sq=sq, c=c: e.matmul(self.PS[bank][:, :], self.ONES, sq, start=(c == 0), stop=(c == 3)),
                      reads=[('smh', 2, c % 2), 'CB'], writes=[('ps', bank)])
            R = self.SM[1]
            tr.op('dve', lambda e: e.tensor_scalar(R[:], self.PS[bank][:, :], 1.0 / BW, EPS, ALU.mult, ALU.add),
                  reads=[('ps', bank)], writes=[('sm', 1)])
            self.rsqrt_inplace(R[:], ('sm', 1))
            for c in range(4):
                gcol = self.VT[:, 200 * l + 192 + c:200 * l + 193 + c]
                tr.op('dve', lambda e, c=c, gcol=gcol: e.scalar_tensor_tensor(
                    W3[:, 4 + c, ts], W3[:, 4 + c, ts], gcol, R[:], ALU.mult, ALU.mult),
                    reads=[('wk', 4 + c, tt), ('sm', 1), 'VT'], writes=[('wk', 4 + c, tt)])
        for n in range(T // 128):
            tt = n // 4
            ns = slice(n * 128, (n + 1) * 128)
            b = tr.bank()
            pbv = self.psb(b)

            def fnt(e, pbv=pbv, ns=ns):
                ins = None
                for gg in range(4):
                    ins = e.transpose(pbv[:, gg * 128:(gg + 1) * 128], W3[:, 4 + gg, ns], self.IDB)
                return ins
            tr.op('pe', fnt, reads=[('wk', 4 + gg, tt) for gg in range(4)] + ['CB'], writes=[('ps', b)])
            vt = self.SM[3][:].bitcast(BF16)[:, (n % 2) * 512:(n % 2 + 1) * 512]
            vk = ('smh', 3, n % 2)
            tr.op('act', lambda e, vt=vt, pbv=pbv: e.copy(vt, pbv[:, 0:512]), reads=[('ps', b)], writes=[vk])
            b2 = tr.bank()

            def fnm(e, b2=b2, vt=vt):
                ins = None
                for gg in range(4):
                    ins = e.matmul(self.PS[b2][:, gg * 128:(gg + 1) * 128], vt[:, gg * 128:(gg + 1) * 128],
                                   self.WSB[:, gg, :], start=True, stop=True)
                return ins
            tr.op('pe', fnm, reads=[vk, 'WSB'], writes=[('ps', b2)])
            tmp = self.SM[4]
            tr.op('dve', lambda e, b2=b2: e.tensor_tensor(tmp[:], self.PS[b2][:, :], BSB[:], ALU.add),
                  reads=[('ps', b2), ('sm', 0)], writes=[('sm', 4)])
            tr.op('dve', lambda e, ns=ns: e.tensor_tensor(B3[:, 4:8, ns], tmp[:].rearrange("p (g q) -> p g q", g=4),
                                                          W3[:, 0:4, ns], ALU.mult),
                  reads=[('sm', 4)] + [('wk', c, tt) for c in range(4)], writes=[('br', c, tt) for c in range(4, 8)])
            self.ada_tick(2)

    def fourier(self, g, l, T):
        tr = self.tr
        NT = T // 512
        H3 = self.v3(self.HT, T)
        W3 = self.v3(self.WK, T)
        B3 = self.v3(self.BR, T)
        hkey = lambda kc, tt: ('h', kc, tt)

        def consumer(oc, tt, bank):
            ts = slice(tt * 512, (tt + 1) * 512)
            tr.op('act', lambda e: e.copy(W3[:, 8 + oc, ts], self.PS[bank][:, :]), reads=[('ps', bank)],
                  writes=[('wk', 8 + oc, tt)])
        self.linear(self.w_in_piece(l, 5 * BW), 4, H3, hkey, NT, consumer)
        FCC = self.SM[0][:].bitcast(BF16)[:, 0:256]
        tr.dma('pool', [lambda e: e.dma_start(out=FCC, in_=self.c_fcc[:, :])], writes=[('sm', 0)])
        N = 256 if g == 0 else 1024
        nb = N // 128
        nseq = T // N
        half_cols = min(N, 512)
        nhalf = N // half_cols
        FN = [self.WK[:, k * nb * half_cols:(k + 1) * nb * half_cols].rearrange("p (b n) -> p b n", b=nb) for k in range(2)]
        fnkeys = [('wk', c, tt) for c in range(0, 8) for tt in range(NT)]
        for hf in range(nhalf):
            tr.dma('pool', [lambda e, k=k, hf=hf: e.dma_start(
                out=FN[k], in_=self.c_fn[g][:, k, :, hf * half_cols:(hf + 1) * half_cols]) for k in range(2)],
                writes=fnkeys)
            for s in range(nseq):
                for gg in range(4):
                    AB = self.WK[:, 12 * T + (gg % 2) * nb * 256: 12 * T + (gg % 2 + 1) * nb * 256].rearrange(
                        "p (b f) -> p b f", b=nb)
                    abk = ('ab', gg % 2)
                    abkeys = [('wk', c, tt) for c in range(12, 16) for tt in range(NT)]
                    for blk0 in range(0, nb, 2):
                        b = tr.bank()

                        def fn1(e, b=b, blk0=blk0, gg=gg, s=s):
                            ins = None
                            for q in range(2):
                                t0 = s * N + (blk0 + q) * 128
                                ins = e.matmul(self.PS[b][:, q * 256:(q + 1) * 256], W3[:, 8 + gg, t0:t0 + 128], FCC,
                                               start=True, stop=True)
                            return ins
                        tr.op('pe', fn1, reads=[('wk', 8 + gg, tt) for tt in range(NT)] + [('sm', 0)], writes=[('ps', b)])
                        tr.op('act', lambda e, b=b, blk0=blk0, AB=AB: e.copy(
                            AB[:, blk0:blk0 + 2, :], self.PS[b][:, :].rearrange("p (q f) -> p q f", q=2)),
                            reads=[('ps', b)], writes=[abk] + abkeys)
                    b2 = tr.bank()

                    def fn2(e, b2=b2, AB=AB):
                        ins = None
                        tot = 2 * nb
                        n = 0
                        for blk in range(nb):
                            for k in range(2):
                                ins = e.matmul(self.PS[b2][:, 0:half_cols], AB[:, blk, k * 128:(k + 1) * 128], FN[k][:, blk, :],
                                               start=(n == 0), stop=(n == tot - 1))
                                n += 1
                        return ins
                    tr.op('pe', fn2, reads=[abk] + fnkeys, writes=[('ps', b2)])
                    t0 = s * N + hf * half_cols
                    tr.op('dve', lambda e, b2=b2, gg=gg, t0=t0: e.tensor_copy(B3[:, 8 + gg, t0:t0 + half_cols],
                                                                            self.PS[b2][:, 0:half_cols]),
                          reads=[('ps', b2)], writes=[('br', 8 + gg, t0 // 512)])
                    self.ada_tick(2)

    def pool(self, g, l, T):
        tr = self.tr
        NT = T // 512
        H3 = self.v3(self.HT, T)
        W3 = self.v3(self.WK, T)
        B3 = self.v3(self.BR, T)
        hkey = lambda kc, tt: ('h', kc, tt)

        def consumer(oc, tt, bank):
            ts = slice(tt * 512, (tt + 1) * 512)
            tr.op('act', lambda e: e.copy(W3[:, 12 + oc, ts], self.PS[bank][:, :]), reads=[('ps', bank)],
                  writes=[('wk', 12 + oc, tt)])
        self.linear(self.w_in_piece(l, 6 * BW), 4, H3, hkey, NT, consumer)
        st = self.SM[0]
        tr.dma('pool', [lambda e: e.dma_start(out=st[:].rearrange("p (g d) -> p g d", g=4),
                                               in_=self.w_pool[l].rearrange("g c d -> c g d"))], writes=[('sm', 0)])
        tr.op('act', lambda e: e.copy(self.WPB[:].rearrange("p g d -> p (g d)"), st[:]), reads=[('sm', 0)], writes=['WPB'])
        N = 256 if g == 0 else 1024
        nb = N // 128
        nseq = T // N
        ntb = T // 128
        XTOK = self.WK[:, 0: ntb * 512].rearrange("p (b f) -> p b f", b=ntb)
        xkeys = [('wk', c, tt) for c in range(0, 4) for tt in range(NT)]
        for n in range(ntb):
            b = tr.bank()
            pbv = self.psb(b)

            def fnt(e, pbv=pbv, n=n):
                ins = None
                for gg in range(4):
                    ins = e.transpose(pbv[:, gg * 128:(gg + 1) * 128], W3[:, 12 + gg, n * 128:(n + 1) * 128], self.IDB)
                return ins
            tr.op('pe', fnt, reads=[('wk', 12 + gg, n // 4) for gg in range(4)] + ['CB'], writes=[('ps', b)])
            tr.op('act', lambda e, pbv=pbv, n=n: e.copy(XTOK[:, n, :], pbv[:, 0:512]), reads=[('ps', b)],
                  writes=[('xtok', n)] + xkeys)
        for gg in range(4):
            PMg = self.SM[1 + (gg % 2)][:].bitcast(BF16)[:, 0:640].rearrange("p (v n) -> p v n", v=5)
            pmk = ('sm', 1 + gg % 2)
            tr.dma('pool', [lambda e, PMg=PMg, gg=gg: e.dma_start(out=PMg, in_=self.c_pm[:, gg, :, :])], writes=[pmk])
            for tt in range(NT):
                b = tr.bank()

                def fnp(e, b=b, tt=tt, gg=gg, PMg=PMg):
                    ins = None
                    for q in range(4):
                        n = tt * 4 + q
                        s, bi = divmod(n, nb)
                        terms = []
                        if bi > 0:
                            terms.append((n - 1, 0))
                        terms.append((n, 1 if bi == 0 else (3 if bi == nb - 1 else 2)))
                        if bi < nb - 1:
                            terms.append((n + 1, 4))
                        for k, (src, var) in enumerate(terms):
                            ins = e.matmul(self.PS[b][:, q * 128:(q + 1) * 128], XTOK[:, src, gg * 128:(gg + 1) * 128],
                                           PMg[:, var, :], start=(k == 0), stop=(k == len(terms) - 1))
                    return ins
                tr.op('pe', fnp, reads=[('xtok', n) for n in range(ntb)] + [pmk], writes=[('ps', b)])
                pl = self.SM[3][:].bitcast(BF16)[:, (tt % 2) * 512:(tt % 2 + 1) * 512]
                plk = ('smh', 3, tt % 2)
                tr.op('act', lambda e, b=b, pl=pl: e.copy(pl, self.PS[b][:, :]), reads=[('ps', b)], writes=[plk])
                b2 = tr.bank()
                tr.op('pe', lambda e, b2=b2, pl=pl, gg=gg: e.matmul(self.PS[b2][:, :], self.WPB[:, gg, :], pl, start=True, stop=True),
                      reads=[plk, 'WPB'], writes=[('ps', b2)])
                scol = self.VT[:, 200 * l + 196 + gg:200 * l + 197 + gg]
                tr.op('dve', lambda e, b2=b2, gg=gg, tt=tt, scol=scol: e.tensor_scalar(
                    B3[:, 12 + gg, tt * 512:(tt + 1) * 512], self.PS[b2][:, :], scol, None, ALU.mult),
                    reads=[('ps', b2), 'VT'], writes=[('br', 12 + gg, tt)])
                self.ada_tick(2)

    def merge(self, l, T):
        tr = self.tr
        NT = T // 512
        H3 = self.v3(self.HT, T)
        W3 = self.v3(self.WK, T)
        B3 = self.v3(self.BR, T)
        order = [(kind, i) for i in range(4) for kind in ('g', 'b')]
        ntiles = [(kind, c, i) for c in range(NCH) for (kind, i) in order]

        def pieces(spec):
            kind, c, i = spec
            if kind == 'g':
                col = 7 * BW + i * D + c * 128
                return [self.w_in[l][:, col:col + 128]]
            return [self.w_branch[l][i * BW:(i + 1) * BW, c * 128:(c + 1) * 128]]

        def gate_ap(i, tt):
            return self.SM[i % 2][:].bitcast(BF16)[:, tt * 512:(tt + 1) * 512], ('smh', i % 2, tt)

        def gate(bs, c, i):
            for tt in range(NT):
                bank = tr.bank()
                self.mm_group(self.PS[bank][:, :], bs, lambda kc, tt=tt: H3[:, kc, tt * 512:(tt + 1) * 512],
                              [('h', kc, tt) for kc in range(NCH)], bank)
                gt, gk = gate_ap(i, tt)
                bcol = self.VT[:, 200 * l + 128 + i * 16 + c:200 * l + 129 + i * 16 + c]
                tr.op('act', lambda e: e.activation(out=gt, in_=self.PS[bank][:, :], func=AF.Sigmoid, bias=bcol, scale=1.0),
                      reads=[('ps', bank), 'VT'], writes=[gk])

        def branch(bsb, c, i):
            for tt in range(NT):
                ts = slice(tt * 512, (tt + 1) * 512)
                acc = self.SM[2 + tt % 2]
                ak = ('sm', 2 + tt % 2)
                tmp = self.SM[4]
                bank = tr.bank()
                self.mm_group(self.PS[bank][:, :], bsb, lambda kc, tt=tt: B3[:, 4 * i + kc, tt * 512:(tt + 1) * 512],
                              [('br', kc, tt) for kc in range(4 * i, 4 * i + 4)], bank, kcs=range(4))
                gt, gk = gate_ap(i, tt)
                if i == 0:
                    tr.op('dve', lambda e: e.tensor_tensor(acc[:], self.PS[bank][:, :], gt, ALU.mult),
                          reads=[('ps', bank), gk], writes=[ak])
                else:
                    tr.op('dve', lambda e: e.tensor_tensor(tmp[:], self.PS[bank][:, :], gt, ALU.mult),
                          reads=[('ps', bank), gk], writes=[('sm', 4)])
                    if i < 3:
                        tr.op('dve', lambda e: e.tensor_tensor(acc[:], acc[:], tmp[:], ALU.add),
                              reads=[ak, ('sm', 4)], writes=[ak])
                    else:
                        tr.op('dve', lambda e: e.tensor_tensor(W3[:, c, ts], acc[:], tmp[:], ALU.add),
                              reads=[ak, ('sm', 4)], writes=[('wk', c, tt)])
        for idx, spec in enumerate(ntiles):
            bs = self.load_w(pieces(spec))
            kind, c, i = spec
            if kind == 'g':
                gate(bs, c, i)
            else:
                branch(bs, c, i)

    def w_out_phase(self, l, i, T):
        tr = self.tr
        NT = T // 512
        W3 = self.v3(self.WK, T)
        X3 = self.v3(self.XT, T)

        def consumer(oc, tt, bank):
            ts = slice(tt * 512, (tt + 1) * 512)
            gcol = self.mvcol(l, 32 + oc, i)
            tr.op('dve', lambda e: e.scalar_tensor_tensor(X3[:, oc, ts], self.PS[bank][:, :], gcol, X3[:, oc, ts],
                                                          ALU.mult, ALU.add),
                  reads=[('ps', bank), ('x', oc, tt), ('MV', l)], writes=[('x', oc, tt)])
        self.linear(lambda oc: [self.w_out[l][:, oc * 128:(oc + 1) * 128]], NCH, W3, lambda kc, tt: ('wk', kc, tt), NT, consumer)

    def mlp(self, l, i, T):
        tr = self.tr
        NT = T // 512
        H3 = self.v3(self.HT, T)
        W3 = self.v3(self.WK, T)
        X3 = self.v3(self.XT, T)
        for qd in range(4):
            def c1(oc, tt, bank):
                ts = slice(tt * 512, (tt + 1) * 512)
                r = self.SM[2 + (oc % 2)]
                rk = ('sm', 2 + oc % 2)
                tr.op('act', lambda e: e.activation(out=r[:], in_=self.PS[bank][:, :], func=AF.Relu),
                      reads=[('ps', bank)], writes=[rk])
                tr.op('pool', lambda e: e.tensor_tensor(W3[:, oc, ts], r[:], r[:], ALU.mult), reads=[rk],
                      writes=[('wk', oc, tt)])
            self.linear(lambda oc, qd=qd: [self.w_mlp1[l][:, qd * D + oc * 128: qd * D + (oc + 1) * 128]], NCH, H3,
                        lambda kc, tt: ('h', kc, tt), NT, c1)

            def c2(oc, tt, bank):
                ts = slice(tt * 512, (tt + 1) * 512)
                gcol = self.mvcol(l, 80 + oc, i)
                tr.op('dve', lambda e: e.scalar_tensor_tensor(X3[:, oc, ts], self.PS[bank][:, :], gcol, X3[:, oc, ts],
                                                              ALU.mult, ALU.add),
                      reads=[('ps', bank), ('x', oc, tt), ('MV', l)], writes=[('x', oc, tt)])
            self.linear(lambda oc, qd=qd: [self.w_mlp2[l][qd * D:(qd + 1) * D, oc * 128:(oc + 1) * 128]], NCH, W3,
                        lambda kc, tt: ('wk', kc, tt), NT, c2)

    def dump(self, name, ap, keys):
        if name in self.dbg:
            self.tr.dma('pool', [lambda e: e.dma_start(out=self.dbg[name], in_=ap)], reads=keys)

    def scoped(self, name, fn, *a):
        if self.scopes:
            with self.nc.named_scope(name):
                fn(*a)
        else:
            fn(*a)

    def build(self):
        tr = self.tr
        self.MASK2 = self.nc.alloc_sbuf_tensor("MASK2", [128, 64], F32)
        tr.dma('pool', [lambda e: e.dma_start(out=self.MASK2[:], in_=self.c_mask2[:, :])], writes=['MASK2'])
        self.set_wslots(True)
        self.scoped('phase0', self.phase0)
        st = self.stage
        for gi, g in enumerate(self.groups):
            T = 512 if g == 0 else 1024
            tr.barrier()
            self.end_segment()
            self.set_wslots(g == 0)
            self.issue_gidx = 0
            if len(self.groups) == 2:
                self.wmode = 'write' if gi == 0 else 'read'
            tr.barrier()
            tr.epoch()
            if st >= 1:
                self.load_x(g, T)
            tr.barrier()
            for l in range(self.depth):
                tr.epoch()
                if gi == 0 and l + 1 < self.depth:
                    self.ada_begin(l + 1)
                    self.ada_bg = l + 1
                if st >= 2:
                    self.norm_mod(l, g, 0, T)
                if st >= 3:
                    self.attention(g, l, T)
                tr.barrier()
                tr.epoch()
                if st >= 4:
                    self.gmlp(g, l, T)
                tr.barrier()
                if st >= 5:
                    self.fourier(g, l, T)
                tr.barrier()
                if st >= 6:
                    self.pool(g, l, T)
                tr.barrier()
                tr.epoch()
                if gi == 0 and l + 1 < self.depth:
                    self.ada_bg = None
                    self.ada_finish(l + 1)
                if st >= 7:
                    self.merge(l, T)
                if st >= 8:
                    self.w_out_phase(l, g, T)
                    self.norm_mod(l, g, 1, T)
                tr.epoch()
                if st >= 9:
                    self.mlp(l, g, T)
                if gi == 0 and l + 1 < self.depth:
                    self.ada_bg = None
                    self.ada_finish(l + 1)
            tr.barrier()
            if st >= 1:
                self.store_y(g, T)
        tr.barrier()
        tr.finish()
        return self.nc


_CONSTS = None


def _prep_inputs(inputs):
    global _CONSTS
    if _CONSTS is None:
        _CONSTS = make_consts()
    f = lambda a: np.ascontiguousarray(np.asarray(a, dtype=np.float32))
    shared = {}
    for k in ("norm1_g", "w_in", "b_gate", "q_norm_g", "k_norm_g", "gmlp_norm_g", "w_spatial", "w_pool",
              "pool_scale", "w_branch", "w_out", "norm2_g", "w_mlp1", "w_mlp2", "w_ada", "b_ada"):
        shared[k] = f(inputs[k])
    shared["b_spatial"] = f(inputs["b_spatial"]).reshape(DEPTH, 512)
    rp = f(inputs["rpb"]).reshape(-1)
    shared["rpb_pad"] = np.concatenate([np.zeros(64, np.float32), rp, np.zeros(64, np.float32)])
    shared.update(_CONSTS)
    xp = f(inputs["x_prompt"])
    xs = f(inputs["x_sample"])
    ck = f(inputs["cache_k"])
    cv = f(inputs["cache_v"])
    c = f(inputs["c"])
    cc = f(inputs["c_ctx"])
    in_maps = []
    for i in range(N_CORES):
        m = dict(shared)
        m["x0"] = np.ascontiguousarray(xp[2 * i:2 * i + 2].reshape(512, D))
        m["x1"] = np.ascontiguousarray(xs[i])
        m["ck"] = np.ascontiguousarray(ck[i])
        m["cv"] = np.ascontiguousarray(cv[i])
        m["cond"] = np.ascontiguousarray(np.stack([cc, c[i]], axis=0))
        in_maps.append(m)
    return in_maps


def kernel(**inputs):
    in_maps = _prep_inputs(inputs)
    dry = Builder()
    dry.build()
    plan = (dry.rec_tiles, dry.rec_segs)
    nc = Builder(plan=plan).build()
    res = run_bass_kernel_spmd(nc, in_maps, core_ids=list(range(N_CORES)))
    r = res.results
    y_prompt = np.concatenate([r[i]["y0"].reshape(2, 256, D) for i in range(N_CORES)], axis=0).astype(np.float32)
    y_sample = np.stack([r[i]["y1"] for i in range(N_CORES)], axis=0).astype(np.float32)
    nk = np.concatenate([r[i]["nk"] for i in range(N_CORES)], axis=0).astype(np.float32)
    nv = np.concatenate([r[i]["nv"] for i in range(N_CORES)], axis=0).astype(np.float32)
    return (y_prompt, y_sample, nk, nv)
```
